# Optimizing a Trainium2 kernel written in Bass

```python
import jax, jax.numpy as jnp
from jax import lax
import numpy as np

D_MODEL = 2048
BATCH = 2
SEQ = 8192
DEPTH = 4

PLE_DIM = 256
D_CONV = D_MODEL // 2
CONV_K = 3
D_REC = D_MODEL // 2
HG_HEAD_DIM = 128
HG_HEADS = D_REC // HG_HEAD_DIM
HG_VDIM = D_REC // HG_HEADS
CHUNK = 64
EPS = 1e-6
LB_FLOOR = 1e-30
IN_SPLITS = [D_CONV, D_CONV, D_CONV, D_CONV,
             D_REC, D_REC, D_REC, D_REC,
             D_MODEL, D_MODEL]
IN_COLS = sum(IN_SPLITS)

kernel_name = "hybrid_shortconv_hgrn2_gated_merge"


def rmsnorm(x, g):
    xf = x.astype(jnp.float32)
    y = xf * lax.rsqrt(jnp.mean(xf * xf, axis=-1, keepdims=True) + EPS)
    return (y * g.astype(jnp.float32)).astype(x.dtype)


def causal_dwconv3(u, w):
    up = jnp.pad(u, ((0, 0), (CONV_K - 1, 0), (0, 0)))
    s = u.shape[1]
    return up[:, 0:s] * w[0] + up[:, 1:s + 1] * w[1] + up[:, 2:s + 2] * w[2]


def hgrn2_chunked(q, k, v, log_f):
    bn, sn, h, kd = q.shape
    vd = v.shape[-1]
    n = sn // CHUNK

    def to_chunks(t):
        return t.reshape(bn, n, CHUNK, h, t.shape[-1]).transpose(1, 0, 3, 2, 4)

    qc, kc, vc, fc = to_chunks(q), to_chunks(k), to_chunks(v), to_chunks(log_f)
    causal = jnp.tril(jnp.ones((CHUNK, CHUNK), dtype=bool))[:, :, None]
    causal_f = causal.astype(jnp.float32)

    def step(state, inp):
        qn, kn, vn, fn = inp
        b = jnp.cumsum(fn, axis=2)
        inter = jnp.einsum('bhtk,bhkv->bhtv', qn * jnp.exp(b), state)
        diff = b[:, :, :, None, :] - b[:, :, None, :, :]
        decay = jnp.exp(jnp.where(causal, diff, 0.0)) * causal_f
        a = jnp.einsum('bhtk,bhtsk,bhsk->bhts', qn, decay, kn)
        o = inter + jnp.einsum('bhts,bhsv->bhtv', a, vn)
        b_last = b[:, :, -1:, :]
        state = (jnp.exp(b_last[:, :, 0, :])[..., None] * state
                 + jnp.einsum('bhsk,bhsv->bhkv', kn * jnp.exp(b_last - b), vn))
        return state, o

    s0 = jnp.zeros((bn, h, kd, vd), jnp.float32)
    _, o = lax.scan(step, s0, (qc, kc, vc, fc))
    return o.transpose(1, 0, 3, 2, 4).reshape(bn, sn, h, vd)


def setup_inputs(seed: int = 0) -> dict:
    key = jax.random.key(seed)
    ks = jax.random.split(key, 16)
    f32 = jnp.float32
    nrm = lambda k, shp: jax.random.normal(k, shp, f32)
    return {
        "x": nrm(ks[0], (BATCH, SEQ, D_MODEL)),
        "p": nrm(ks[1], (DEPTH, BATCH, SEQ, PLE_DIM)),
        "norm_mix_g": 1.0 + 0.02 * nrm(ks[2], (DEPTH, D_MODEL)),
        "w_in": nrm(ks[3], (DEPTH, D_MODEL, IN_COLS)) * D_MODEL ** -0.5,
        "conv_w": nrm(ks[4], (DEPTH, CONV_K, D_CONV)) * CONV_K ** -0.5,
        "lb_param": 0.1 * nrm(ks[5], (DEPTH, D_REC)),
        "hg_norm_g": 1.0 + 0.02 * nrm(ks[6], (DEPTH, D_REC)),
        "w_a_out": nrm(ks[7], (DEPTH, D_CONV, D_MODEL)) * D_CONV ** -0.5,
        "w_b_out": nrm(ks[8], (DEPTH, D_REC, D_MODEL)) * D_REC ** -0.5,
        "w_o": nrm(ks[9], (DEPTH, D_MODEL, D_MODEL)) * D_MODEL ** -0.5,
        "ple_norm_g": 1.0 + 0.02 * nrm(ks[10], (DEPTH, D_MODEL)),
        "w_ple_gate": nrm(ks[11], (DEPTH, D_MODEL, D_MODEL)) * D_MODEL ** -0.5,
        "w_ple_proj": nrm(ks[12], (DEPTH, PLE_DIM, D_MODEL)) * PLE_DIM ** -0.5,
        "final_norm_g": 1.0 + 0.02 * nrm(ks[13], (D_MODEL,)),
    }


def reference(x, p, norm_mix_g, w_in, conv_w, lb_param, hg_norm_g, w_a_out, w_b_out,
              w_o, ple_norm_g, w_ple_gate, w_ple_proj, final_norm_g):
    dt = x.dtype
    bn, sn, _ = x.shape
    lb_sm = jax.nn.softmax(lb_param.astype(jnp.float32), axis=0)
    lb_all = jnp.cumsum(lb_sm, axis=0) - lb_sm[0]
    split_idx = [int(s) for s in np.cumsum(IN_SPLITS)[:-1]]

    for i in range(DEPTH):
        h = rmsnorm(x, norm_mix_g[i])
        u = h @ w_in[i]
        c_g, b_g, xa, za, q, fl, iv, og, ga, gb = jnp.split(u, split_idx, axis=-1)

        ya = b_g * causal_dwconv3(c_g * xa, conv_w[i]) * jax.nn.silu(za)

        lb = jnp.clip(lb_all[i], 0.0, 1.0 - 1e-6)
        flf = fl.astype(jnp.float32)
        log_f = jnp.logaddexp(jnp.log(jnp.maximum(lb, LB_FLOOR)),
                              jnp.log1p(-lb) + jax.nn.log_sigmoid(flf))
        kk = (1.0 - lb) * jax.nn.sigmoid(-flf)
        hshape = (bn, sn, HG_HEADS, HG_HEAD_DIM)
        o = hgrn2_chunked(jax.nn.silu(q.astype(jnp.float32)).reshape(hshape),
                          kk.reshape(hshape),
                          iv.astype(jnp.float32).reshape(bn, sn, HG_HEADS, HG_VDIM),
                          log_f.reshape(hshape))
        o = o * lax.rsqrt(jnp.mean(o * o, axis=-1, keepdims=True) + EPS)
        o = o.reshape(bn, sn, D_REC) * hg_norm_g[i].astype(jnp.float32)
        yb = o.astype(dt) * jax.nn.silu(og)

        m = jax.nn.sigmoid(ga) * (ya @ w_a_out[i]) + jax.nn.sigmoid(gb) * (yb @ w_b_out[i])
        x = x + m @ w_o[i]

        gate = jax.nn.sigmoid(rmsnorm(x, ple_norm_g[i]) @ w_ple_gate[i])
        x = x + gate * (p[i] @ w_ple_proj[i])

    return rmsnorm(x, final_norm_g)
```

```python
import contextlib
import numpy as np
import concourse.bass as bass
import concourse.mybir as mybir
from concourse.bass_utils import run_bass_kernel_spmd

F32 = mybir.dt.float32
BF16 = mybir.dt.bfloat16
AF = mybir.ActivationFunctionType
ALU = mybir.AluOpType

D = 2048
NKC = 16
PLE = 256
DC = 1024
DR = 1024
NH = 8
DEPTH = 4
TT = 512
EPS = 1e-6
GW = 8192
TOT = 8 * GW * 2 + 16 * 6144 + 4 * GW * 2 + 4096
EPOCH = 16000


class Prog:
    ENGINES = ("pe", "act", "dve", "pool", "sp")

    def __init__(self, nc):
        self.nc = nc
        self.ops = []
        self.last_w = {}
        self.readers = {}
        self.lane_count = {}

    def op(self, eng, emit, reads=(), writes=(), lane=None):
        idx = len(self.ops)
        is_dma = lane is not None
        deps = set()
        for k in reads:
            w = self.last_w.get(k)
            if w is not None:
                deps.add(w)
        for k in writes:
            w = self.last_w.get(k)
            if w is not None:
                deps.add(w)
            for r in self.readers.get(k, {}).values():
                deps.add(r)
        need = []
        for d in deps:
            p = self.ops[d]
            if p["dma"] or is_dma or p["eng"] != eng:
                need.append(d)
            else:
                raw = any(self.last_w.get(k) == d for k in reads)
                if raw and eng != "pe":
                    need.append(d)
        for d in need:
            self.ops[d]["signal"] = True
        rec = dict(eng=eng, emit=emit, dma=is_dma, lane=lane, need=sorted(need), signal=False)
        if is_dma:
            self.lane_count[lane] = self.lane_count.get(lane, 0) + 1
            rec["lane_idx"] = self.lane_count[lane]
        self.ops.append(rec)
        rkey = ("dma", lane) if is_dma else eng
        for k in reads:
            self.readers.setdefault(k, {})[rkey] = idx
        for k in writes:
            self.last_w[k] = idx
            self.readers[k] = {}
        return idx

    def emit_all(self, final_wait_ops=()):
        nc = self.nc
        cnt = {e: 0 for e in self.ENGINES}
        for o in self.ops:
            if o["dma"]:
                o["tok"] = (("lane", o["lane"]), 16 * o["lane_idx"])
            elif o["signal"]:
                c = cnt[o["eng"]]
                cnt[o["eng"]] = c + 1
                o["tok"] = (("eng", o["eng"], c // EPOCH), c % EPOCH + 1)
        semnames = []
        seen = set()
        for o in self.ops:
            if "tok" in o and o["tok"][0] not in seen:
                seen.add(o["tok"][0])
                semnames.append(o["tok"][0])
        sems = {}
        with contextlib.ExitStack() as st:
            for sn in semnames:
                sems[sn] = st.enter_context(nc.semaphore("s_" + "_".join(str(x) for x in sn)))
            block = st.enter_context(nc.Block())
            by_eng = {e: [o for o in self.ops if o["eng"] == e] for e in self.ENGINES}

            def run(engname, e):
                waited = {}
                maxep = {}
                for o in by_eng[engname]:
                    for d in o["need"]:
                        sn, val = self.ops[d]["tok"]
                        if waited.get(sn, 0) >= val:
                            continue
                        if sn[0] == "eng" and maxep.get(sn[1], -1) > sn[2]:
                            continue
                        e.wait_ge(sems[sn], val)
                        waited[sn] = val
                        if sn[0] == "eng":
                            maxep[sn[1]] = max(maxep.get(sn[1], -1), sn[2])
                    ins = o["emit"](e)
                    if "tok" in o:
                        sn, val = o["tok"]
                        ins.then_inc(sems[sn], 16 if o["dma"] else 1)
                if engname == "sp":
                    for d in final_wait_ops:
                        sn, val = self.ops[d]["tok"]
                        if waited.get(sn, 0) < val:
                            e.wait_ge(sems[sn], val)
                            waited[sn] = val

            @block.tensor
            def _(e):
                run("pe", e)

            @block.scalar
            def _(e):
                run("act", e)

            @block.vector
            def _(e):
                run("dve", e)

            @block.gpsimd
            def _(e):
                run("pool", e)

            @block.sync
            def _(e):
                run("sp", e)


def _kmajor(w):
    K, N = w.shape
    return np.ascontiguousarray(w.reshape(K // 128, 128, N).transpose(1, 0, 2)).reshape(128, -1)


def pack_layer(w_in, w_a, w_b, w_o, w_pg, w_pp):
    out = np.empty((128, TOT), np.float32)
    off = 0
    r = np.arange(128)
    for cb in range(8):
        cols = np.concatenate([j * 1024 + cb * 128 + r for j in range(4)])
        out[:, off:off + GW] = _kmajor(w_in[:, cols]); off += GW
    for hd in range(8):
        cols = np.concatenate([4096 + hd * 128 + r, 7168 + hd * 128 + r,
                               5120 + hd * 128 + r, 6144 + hd * 128 + r])
        out[:, off:off + GW] = _kmajor(w_in[:, cols]); off += GW
    for ob in range(16):
        cols = np.concatenate([8192 + ob * 128 + r, 10240 + ob * 128 + r])
        out[:, off:off + 4096] = _kmajor(w_in[:, cols]); off += 4096
        out[:, off:off + 1024] = _kmajor(w_a[:, ob * 128:(ob + 1) * 128]); off += 1024
        out[:, off:off + 1024] = _kmajor(w_b[:, ob * 128:(ob + 1) * 128]); off += 1024
    for g in range(4):
        out[:, off:off + GW] = _kmajor(w_o[:, g * 512:(g + 1) * 512]); off += GW
    for g in range(4):
        out[:, off:off + GW] = _kmajor(w_pg[:, g * 512:(g + 1) * 512]); off += GW
    out[:, off:off + 4096] = _kmajor(w_pp); off += 4096
    assert off == TOT
    return out


def group_table():
    g = []
    off = 0
    for cb in range(8):
        g.append(("A", cb, off, GW)); off += GW
    for hd in range(8):
        g.append(("B", hd, off, GW)); off += GW
    for ob in range(16):
        g.append(("M", ob, off, 6144)); off += 6144
    for k in range(4):
        g.append(("O", k, off, GW)); off += GW
    for k in range(4):
        g.append(("G", k, off, GW)); off += GW
    pp = ("PP", 0, off, 4096)
    return g, pp


def host_consts():
    s = np.arange(128)
    same = (s[:, None] // 64) == (s[None, :] // 64)
    md = np.where(same & (s[:, None] > s[None, :]), -1.0, 0.0).astype(np.float32)
    mask = np.where(same & (s[:, None] <= s[None, :]), 1.0, 0.0).astype(np.float32)
    csum = np.zeros((128, 2), np.float32)
    csum[:64, 0] = 1.0
    csum[64:, 1] = 1.0
    ones = np.ones((128, 128), np.float32)
    return np.concatenate([ones, md, mask, csum], axis=1)


def fm(v, n):
    return np.ascontiguousarray(np.asarray(v, np.float32).reshape(n, 128).T)


def build_program(n_layers, n_tiles, nwb=2, final_norm_all=True, xs_internal=False):
    T = n_tiles * TT
    nc = bass.Bass("TRN2", target_bir_lowering=False)
    dt_in = lambda name, shape: nc.dram_tensor(name, shape, F32, kind="ExternalInput").ap()
    dt_out = lambda name, shape: nc.dram_tensor(name, shape, F32, kind="ExternalOutput").ap()
    xT_d = dt_in("xT", [D, T])
    pT_d = dt_in("pT", [n_layers, PLE, T])
    wpk_d = dt_in("wpk", [n_layers, 128, TOT])
    smallp_d = dt_in("smallp", [n_layers, 128, 64])
    fng_d = dt_in("fng", [128, 16])
    lbfm_d = dt_in("lbp_fm", [128, 32])
    lbbc_d = dt_in("lbp_bc", [128, 4 * DR])
    cmask_d = dt_in("cmask", [n_layers, 128, 4])
    consts_d = dt_in("consts", [128, 386])
    sin_d = dt_in("s_in", [128, n_layers * NH * 128])
    hin_d = dt_in("halo_in", [128, n_layers * 16])
    if xs_internal:
        xo_d = nc.dram_tensor("xs", [D, T], F32, kind="Internal").ap()
    else:
        xo_d = dt_out("xo", [D, T])
    yo_d = dt_out("yo", [D, T])
    sout_d = dt_out("s_out", [128, n_layers * NH * 128])
    hout_d = dt_out("halo_out", [128, n_layers * 16])

    P = Prog(nc)
    st = contextlib.ExitStack()
    sb = lambda name, shape, dt=F32: st.enter_context(nc.sbuf_tensor(name, shape, dt))

    xt = sb("xt", [128, NKC, TT])
    hT = sb("hT", [128, NKC, TT], BF16)
    yaT = sb("yaT", [128, 8, TT], BF16)
    ybT = sb("ybT", [128, 8, TT], BF16)
    mT = sb("mT", [128, NKC, TT], BF16)
    pTt = sb("pTt", [128, 2, TT], BF16)
    wbufs = [sb("wb%d" % i, [128, GW], BF16) for i in range(nwb)]
    wpp = sb("wpp", [128, 4096], BF16)
    NSLOT = 18
    SLW = 520
    arena = sb("arena", [128, NSLOT, SLW])
    bfa = sb("bfa", [128, 4, TT], BF16)
    atm = sb("atm", [128, 2, 128], BF16)
    S = sb("S", [128, n_layers * NH, 128])
    halo = sb("halo", [128, n_layers * 8, 2])
    smallp = sb("smallp_sb", [128, n_layers, 64])
    fng = sb("fng_sb", [128, 16])
    consts = sb("consts_sb", [128, 386])
    lbfm = sb("lbfm", [128, 4, 8])
    cmask = sb("cmask_sb", [128, n_layers, 4])
    lbq = sb("lbq", [128, 6, 8])
    lbB = sb("lbB", [128, DR])
    omlB = sb("omlB", [128, DR])
    eb = sb("eb", [128, 8])
    epsb = sb("epsb", [128, 1])
    ps = [st.enter_context(nc.psum_tensor("ps%d" % i, [128, TT], F32)) for i in range(8)]

    ones_c = consts[:, 0:128]
    md_c = consts[:, 128:256]
    mask_c = consts[:, 256:384]
    csum_c = consts[:, 384:386]

    def mm(out, lhsT, rhs, start, stop, reads, writes):
        return P.op("pe", lambda e: e.matmul(out, lhsT=lhsT, rhs=rhs, start=start, stop=stop),
                    reads, writes)

    def act(out, in_, func, reads, writes, bias=None, scale=None):
        kw = {}
        if bias is not None:
            kw["bias"] = bias
        if scale is not None:
            kw["scale"] = scale
        return P.op("act", lambda e: e.activation(out=out, in_=in_, func=func, **kw), reads, writes)

    def tt(out, in0, in1, op, reads, writes):
        return P.op("dve", lambda e: e.tensor_tensor(out=out, in0=in0, in1=in1, op=op), reads, writes)

    def ts(out, in0, s1, s2, op0, op1, reads, writes):
        if s2 is None:
            return P.op("dve", lambda e: e.tensor_scalar(out=out, in0=in0, scalar1=s1, scalar2=None, op0=op0),
                        reads, writes)
        return P.op("dve", lambda e: e.tensor_scalar(out=out, in0=in0, scalar1=s1, scalar2=s2, op0=op0, op1=op1),
                    reads, writes)

    def stt(out, in0, scalar, in1, op0, op1, reads, writes):
        return P.op("dve", lambda e: e.scalar_tensor_tensor(out=out, in0=in0, scalar=scalar, in1=in1,
                                                            op0=op0, op1=op1), reads, writes)

    def recip(out, in_, reads, writes):
        return P.op("dve", lambda e: e.reciprocal(out=out, in_=in_), reads, writes)

    def dma(q, out, in_, reads, writes, lane):
        return P.op(q, lambda e: e.dma_start(out=out, in_=in_), reads, writes, lane=lane)

    def slot(i, w=TT):
        return arena[:, i, 0:w]

    SK = lambda i: ("sl", i)
    PSK = lambda i: ("ps", i)

    dma("sp", consts[:], consts_d[:, :], [], ["consts"], "c0")
    dma("sp", smallp[:], smallp_d.rearrange("l p n -> p l n"), [], ["smallp"], "c1")
    dma("sp", fng[:], fng_d[:, :], [], ["fng"], "c2")
    dma("sp", lbfm[:], lbfm_d.rearrange("p (l h) -> p l h", l=4), [], ["lbfm"], "c3")
    dma("sp", cmask[:], cmask_d.rearrange("l p n -> p l n"), [], ["cmask"], "c4")
    dma("sp", S[:], sin_d.rearrange("p (a v) -> p a v", v=128), [],
        [("S", l, h) for l in range(n_layers) for h in range(NH)], "c5")
    dma("sp", halo[:], hin_d.rearrange("p (a v) -> p a v", v=2), [],
        [("halo", l, c) for l in range(n_layers) for c in range(8)], "c6")
    P.op("dve", lambda e: e.memset(epsb[:], EPS), [], ["epsb"])

    gtab, ppg = group_table()
    stream = []
    for l in range(n_layers):
        for t in range(n_tiles):
            for (kind, idx, off, w) in gtab:
                stream.append((l, t, kind, idx, off, w))
    issued = [0]

    def issue_upto(n):
        while issued[0] < min(n, len(stream)):
            i = issued[0]
            l, t, kind, idx, off, w = stream[i]
            b = i % nwb
            dma("pool", wbufs[b][:, 0:w], wpk_d[l, :, off:off + w], [], [("wb", b)], "w%d" % b)
            issued[0] += 1

    gpos = [0]

    def next_group():
        i = gpos[0]
        issue_upto(i + nwb)
        gpos[0] += 1
        return wbufs[i % nwb], ("wb", i % nwb)

    def emit_lb(l):
        e4 = arena[:, 0, 0:32].rearrange("p (l h) -> p l h", l=4)
        act(e4, lbfm[:], AF.Exp, ["lbfm"], [SK(0)])
        ssum, num, lb_, oml_, noml_, tmp = [lbq[:, i, :] for i in range(6)]
        tt(ssum, e4[:, 0, :], e4[:, 1, :], ALU.add, [SK(0)], ["lbq0"])
        tt(ssum, ssum, e4[:, 2, :], ALU.add, [SK(0), "lbq0"], ["lbq0"])
        tt(ssum, ssum, e4[:, 3, :], ALU.add, [SK(0), "lbq0"], ["lbq0"])
        recip(ssum, ssum, ["lbq0"], ["lbq0"])
        ts(num, e4[:, 1, :], cmask[:, l, 1:2], None, ALU.mult, None, [SK(0), "cmask"], ["lbq1"])
        stt(num, e4[:, 2, :], cmask[:, l, 2:3], num, ALU.mult, ALU.add, [SK(0), "cmask", "lbq1"], ["lbq1"])
        stt(num, e4[:, 3, :], cmask[:, l, 3:4], num, ALU.mult, ALU.add, [SK(0), "cmask", "lbq1"], ["lbq1"])
        tt(lb_, num, ssum, ALU.mult, ["lbq0", "lbq1"], ["lbq2"])
        ts(lb_, lb_, 0.0, 1.0 - 1e-6, ALU.max, ALU.min, ["lbq2"], ["lbq2"])
        ts(oml_, lb_, -1.0, 1.0, ALU.mult, ALU.add, ["lbq2"], ["lbq3"])
        ts(noml_, oml_, -1.0, None, ALU.mult, None, ["lbq3"], ["lbq4"])
        big = arena[:, 2:10, :].rearrange("p a b -> p (a b)")[:, 0:4 * DR].rearrange("p (l c) -> p l c", l=4)
        bk = [SK(i) for i in range(2, 10)]
        dma("sp", big, lbbc_d.rearrange("p (l c) -> p l c", l=4), [], bk, "lbbc")
        act(big, big, AF.Exp, bk, bk)
        sB = arena[:, 10:12, :].rearrange("p a b -> p (a b)")[:, 0:DR]
        nB = arena[:, 12:14, :].rearrange("p a b -> p (a b)")[:, 0:DR]
        sk = [SK(10), SK(11)]
        nk = [SK(12), SK(13)]
        tt(sB, big[:, 0, :], big[:, 1, :], ALU.add, bk, sk)
        tt(sB, sB, big[:, 2, :], ALU.add, bk + sk, sk)
        tt(sB, sB, big[:, 3, :], ALU.add, bk + sk, sk)
        recip(sB, sB, sk, sk)
        ts(nB, big[:, 1, :], cmask[:, l, 1:2], None, ALU.mult, None, bk + ["cmask"], nk)
        stt(nB, big[:, 2, :], cmask[:, l, 2:3], nB, ALU.mult, ALU.add, bk + nk + ["cmask"], nk)
        stt(nB, big[:, 3, :], cmask[:, l, 3:4], nB, ALU.mult, ALU.add, bk + nk + ["cmask"], nk)
        tt(lbB[:], nB, sB, ALU.mult, sk + nk, ["lbB"])
        ts(lbB[:], lbB[:], 0.0, 1.0 - 1e-6, ALU.max, ALU.min, ["lbB"], ["lbB"])
        ts(omlB[:], lbB[:], -1.0, 1.0, ALU.mult, ALU.add, ["lbB"], ["omlB"])

    def emit_norm(gain_ap, out_fn, out_keys_fn):
        for c in range(NKC):
            s = c % 2
            act(slot(s), xt[:, c, :], AF.Square, [("x", c)], [SK(s)])
            mm(ps[7][:], ones_c, slot(s), c == 0, c == NKC - 1, [SK(s), "consts"], [PSK(7)])
        act(slot(2), ps[7][:], AF.Sqrt, [PSK(7), "epsb"], [SK(2)], bias=epsb[:, 0:1], scale=1.0 / D)
        recip(slot(3), slot(2), [SK(2)], [SK(3)])
        for c in range(NKC):
            stt(out_fn(c), xt[:, c, :], gain_ap[:, c:c + 1], slot(3), ALU.mult, ALU.mult,
                [("x", c), SK(3), "smallp", "fng"], out_keys_fn(c))

    def emit_tile(l, t, x_src, x_dst, last_layer):
        tok = slice(t * TT, (t + 1) * TT)
        sp_l = smallp[:, l, :]
        g_mix = sp_l[:, 0:16]
        g_ple = sp_l[:, 16:32]
        convw = sp_l[:, 32:56].rearrange("p (k c) -> p k c", k=3)
        g_hn = sp_l[:, 56:64]
        xkeys = [("x", c) for c in range(NKC)]
        hkeys = [("h", c) for c in range(NKC)]
        dma("sp", xt[:], x_src.rearrange("(c p) t -> p c t", p=128)[:, :, tok], [("xd", t)], xkeys, "xin")
        dma("pool", pTt[:], pT_d[l].rearrange("(c p) t -> p c t", p=128)[:, :, tok], [], ["pT"], "pin")
        emit_norm(g_mix, lambda c: hT[:, c, :], lambda c: [("h", c)])

        for cb in range(8):
            wb, wk = next_group()
            wv = wb[:, :].rearrange("p (k n) -> p k n", n=512)
            b0 = 0 if cb % 2 == 0 else 4
            so = 0 if cb % 2 == 0 else 4
            sC, sX, sT, sZ = 4 + so, 5 + so, 6 + so, 7 + so
            for j in range(4):
                for kc in range(NKC):
                    mm(ps[b0 + j][:], wv[:, kc, j * 128:(j + 1) * 128], hT[:, kc, :], kc == 0, kc == NKC - 1,
                       [wk, ("h", kc)], [PSK(b0 + j)])
            cx = arena[:, sX, 0:TT + 2]
            act(slot(sC), ps[b0 + 0][:], AF.Copy, [PSK(b0)], [SK(sC)])
            act(cx[:, 0:2], halo[:, l * 8 + cb, :], AF.Copy, [("halo", l, cb)], [SK(sX)])
            tt(cx[:, 2:TT + 2], slot(sC), ps[b0 + 2][:], ALU.mult, [SK(sC), PSK(b0 + 2)], [SK(sX)])
            act(halo[:, l * 8 + cb, :], cx[:, TT:TT + 2], AF.Copy, [SK(sX)], [("halo", l, cb)])
            ts(slot(sT), cx[:, 0:TT], convw[:, 0, cb:cb + 1], None, ALU.mult, None, [SK(sX), "smallp"], [SK(sT)])
            stt(slot(sT), cx[:, 1:TT + 1], convw[:, 1, cb:cb + 1], slot(sT), ALU.mult, ALU.add,
                [SK(sX), SK(sT), "smallp"], [SK(sT)])
            stt(slot(sT), cx[:, 2:TT + 2], convw[:, 2, cb:cb + 1], slot(sT), ALU.mult, ALU.add,
                [SK(sX), SK(sT), "smallp"], [SK(sT)])
            act(slot(sZ), ps[b0 + 3][:], AF.Silu, [PSK(b0 + 3)], [SK(sZ)])
            tt(slot(sT), slot(sT), slot(sZ), ALU.mult, [SK(sT), SK(sZ)], [SK(sT)])
            tt(yaT[:, cb, :], slot(sT), ps[b0 + 1][:], ALU.mult, [SK(sT), PSK(b0 + 1)], [("ya", cb)])

        for hd in range(NH):
            wb, wk = next_group()
            wv = wb[:, :].rearrange("p (k n) -> p k n", n=512)
            hs = slice(hd * 128, (hd + 1) * 128)
            for j, bank in ((0, 0), (1, 2), (2, 1)):
                for kc in range(NKC):
                    mm(ps[bank][:], wv[:, kc, j * 128:(j + 1) * 128], hT[:, kc, :], kc == 0, kc == NKC - 1,
                       [wk, ("h", kc)], [PSK(bank)])
            for tb in range(4):
                bank = 3 + tb // 2
                o_ap = ps[bank][:, (tb % 2) * 256:(tb % 2) * 256 + 256]
                for kc in range(NKC):
                    mm(o_ap, hT[:, kc, tb * 128:(tb + 1) * 128], wv[:, kc, 256:512], kc == 0, kc == NKC - 1,
                       [wk, ("h", kc)], [PSK(bank)])
            act(slot(0), ps[0][:], AF.Silu, [PSK(0)], [SK(0)])
            act(slot(1), ps[2][:], AF.Silu, [PSK(2)], [SK(1)])
            act(slot(2), ps[1][:], AF.Sigmoid, [PSK(1)], [SK(2)])
            ts(slot(2), slot(2), lbq[:, 4, hd:hd + 1], lbq[:, 3, hd:hd + 1], ALU.mult, ALU.add,
               [SK(2), "lbq3", "lbq4"], [SK(2)])
            sig_tok = slot(6).rearrange("p (a k) -> p a k", a=4)
            f_tok = slot(7).rearrange("p (a k) -> p a k", a=4)
            k_tok = slot(8).rearrange("p (a k) -> p a k", a=4)
            ek_tok = slot(9).rearrange("p (a k) -> p a k", a=4)
            kt_bf = bfa[:, 2, :].rearrange("p (a k) -> p a k", a=4)
            v_bf = bfa[:, 3, :].rearrange("p (a k) -> p a k", a=4)
            for half in range(2):
                pv = ps[3 + half][:, :].rearrange("p (a c) -> p a c", c=256)
                act(sig_tok[:, 2 * half:2 * half + 2, :], pv[:, :, 0:128], AF.Sigmoid, [PSK(3 + half)], [SK(6)])
                act(v_bf[:, 2 * half:2 * half + 2, :], pv[:, :, 128:256], AF.Copy, [PSK(3 + half)], [("bfa", 3)])
            omlB_b = omlB[:, hs].unsqueeze(1).broadcast_to([128, 4, 128])
            lbB_b = lbB[:, hs].unsqueeze(1).broadcast_to([128, 4, 128])
            tt(sig_tok, sig_tok, omlB_b, ALU.mult, [SK(6), "omlB"], [SK(6)])
            tt(f_tok, sig_tok, lbB_b, ALU.add, [SK(6), "lbB"], [SK(7)])
            stt(k_tok, sig_tok, -1.0, omlB_b, ALU.mult, ALU.add, [SK(6), "omlB"], [SK(8)])
            act(f_tok, f_tok, AF.Ln, [SK(7)], [SK(7)])
            for tb in range(4):
                mm(ps[5][:, tb * 128:(tb + 1) * 128], md_c, f_tok[:, tb, :], True, True,
                   [SK(7), "consts"], [PSK(5)])
            for tb in range(4):
                mm(ps[6][:, tb * 128:(tb + 1) * 128], f_tok[:, tb, :], md_c, True, True,
                   [SK(7), "consts"], [PSK(6)])
            for tb in range(4):
                mm(ps[7][:, tb * 2:tb * 2 + 2], f_tok[:, tb, :], csum_c, True, True,
                   [SK(7), "consts"], [PSK(7)])
            act(slot(9), ps[5][:], AF.Exp, [PSK(5)], [SK(9)], scale=-1.0)
            tt(kt_bf, k_tok, ek_tok, ALU.mult, [SK(8), SK(9)], [("bfa", 2)])
            act(slot(3), ps[6][:], AF.Exp, [PSK(6)], [SK(3)])
            act(slot(4), ps[6][:], AF.Exp, [PSK(6)], [SK(4)], scale=-1.0)
            act(eb[:], ps[7][:, 0:8], AF.Exp, [PSK(7)], ["eb"])
            tt(slot(3), slot(0), slot(3), ALU.mult, [SK(0), SK(3)], [SK(3)])
            act(bfa[:, 0, :], slot(3), AF.Copy, [SK(3)], [("bfa", 0)])
            tt(bfa[:, 1, :], slot(2), slot(4), ALU.mult, [SK(2), SK(4)], [("bfa", 1)])
            tt(slot(5).rearrange("p (c j) -> p c j", j=64), slot(3).rearrange("p (c j) -> p c j", j=64),
               eb[:, :].unsqueeze(2).broadcast_to([128, 8, 64]), ALU.mult, [SK(3), "eb"], [SK(5)])
            Sh = S[:, l * NH + hd, :]
            skey = ("S", l, hd)
            for tb in range(4):
                tsl = slice(tb * 128, (tb + 1) * 128)
                a = tb % 2
                mm(ps[1][:, 0:128], bfa[:, 1, tsl], bfa[:, 0, tsl], True, True, [("bfa", 0), ("bfa", 1)], [PSK(1)])
                tt(atm[:, a, :], ps[1][:, 0:128], mask_c, ALU.mult, [PSK(1), "consts"], [("atm", a)])
                mm(ps[0][:, tsl], v_bf[:, tb, :], atm[:, a, :], True, False, [("bfa", 3), ("atm", a)], [PSK(0)])
                for half in range(2):
                    c = tb * 2 + half
                    csl = slice(c * 64, (c + 1) * 64)
                    prt = slice(half * 64, (half + 1) * 64)
                    mm(ps[0][:, csl], Sh, slot(5)[:, csl], False, True, [skey, SK(5)], [PSK(0)])
                    mm(ps[2][:, 0:128], kt_bf[prt, tb, :], v_bf[prt, tb, :], True, True,
                       [("bfa", 2), ("bfa", 3)], [PSK(2)])
                    stt(Sh, Sh, eb[:, c:c + 1], ps[2][:, 0:128], ALU.mult, ALU.add, [skey, "eb", PSK(2)], [skey])
            act(slot(10), ps[0][:], AF.Square, [PSK(0)], [SK(10)])
            mm(ps[7][:], ones_c, slot(10), True, True, [SK(10), "consts"], [PSK(7)])
            act(slot(11), ps[7][:], AF.Sqrt, [PSK(7), "epsb"], [SK(11)], bias=epsb[:, 0:1], scale=1.0 / 128)
            recip(slot(11), slot(11), [SK(11)], [SK(11)])
            stt(slot(12), ps[0][:], g_hn[:, hd:hd + 1], slot(11), ALU.mult, ALU.mult,
                [PSK(0), SK(11), "smallp"], [SK(12)])
            tt(ybT[:, hd, :], slot(12), slot(1), ALU.mult, [SK(12), SK(1)], [("yb", hd)])

        for ob in range(16):
            wb, wk = next_group()
            wg = wb[:, 0:4096].rearrange("p (k n) -> p k n", n=256)
            wa = wb[:, 4096:5120].rearrange("p (k n) -> p k n", n=128)
            wbb = wb[:, 5120:6144].rearrange("p (k n) -> p k n", n=128)
            b0 = 0 if ob % 2 == 0 else 4
            so = 0 if ob % 2 == 0 else 4
            for kc in range(NKC):
                mm(ps[b0][:], wg[:, kc, 0:128], hT[:, kc, :], kc == 0, kc == NKC - 1, [wk, ("h", kc)], [PSK(b0)])
            for kc in range(NKC):
                mm(ps[b0 + 1][:], wg[:, kc, 128:256], hT[:, kc, :], kc == 0, kc == NKC - 1,
                   [wk, ("h", kc)], [PSK(b0 + 1)])
            for kc in range(8):
                mm(ps[b0 + 2][:], wa[:, kc, :], yaT[:, kc, :], kc == 0, kc == 7, [wk, ("ya", kc)], [PSK(b0 + 2)])
            for kc in range(8):
                mm(ps[b0 + 3][:], wbb[:, kc, :], ybT[:, kc, :], kc == 0, kc == 7, [wk, ("yb", kc)], [PSK(b0 + 3)])
            s0, s1 = 4 + so, 5 + so
            act(slot(s0), ps[b0][:], AF.Sigmoid, [PSK(b0)], [SK(s0)])
            act(slot(s1), ps[b0 + 1][:], AF.Sigmoid, [PSK(b0 + 1)], [SK(s1)])
            tt(slot(s0), slot(s0), ps[b0 + 2][:], ALU.mult, [SK(s0), PSK(b0 + 2)], [SK(s0)])
            tt(slot(s1), slot(s1), ps[b0 + 3][:], ALU.mult, [SK(s1), PSK(b0 + 3)], [SK(s1)])
            tt(mT[:, ob, :], slot(s0), slot(s1), ALU.add, [SK(s0), SK(s1)], [("m", ob)])

        for g in range(4):
            wb, wk = next_group()
            wv = wb[:, :].rearrange("p (k n) -> p k n", n=512)
            for j in range(4):
                ob = g * 4 + j
                bank = ob % 8
                for kc in range(NKC):
                    mm(ps[bank][:], wv[:, kc, j * 128:(j + 1) * 128], mT[:, kc, :], kc == 0, kc == NKC - 1,
                       [wk, ("m", kc)], [PSK(bank)])
                tt(xt[:, ob, :], xt[:, ob, :], ps[bank][:], ALU.add, [("x", ob), PSK(bank)], [("x", ob)])

        dma("pool", wpp[:], wpk_d[l, :, ppg[2]:ppg[2] + 4096], [], ["wpp"], "wpp")
        wppv = wpp[:, :].rearrange("p (k n) -> p k n", n=2048)
        emit_norm(g_ple, lambda c: hT[:, c, :], lambda c: [("h", c)])
        for g in range(4):
            wb, wk = next_group()
            wv = wb[:, :].rearrange("p (k n) -> p k n", n=512)
            for j in range(4):
                ob = g * 4 + j
                b0 = 0 if ob % 2 == 0 else 4
                s0 = 4 if ob % 2 == 0 else 8
                for kc in range(NKC):
                    mm(ps[b0][:], wv[:, kc, j * 128:(j + 1) * 128], hT[:, kc, :], kc == 0, kc == NKC - 1,
                       [wk, ("h", kc)], [PSK(b0)])
                for kc in range(2):
                    mm(ps[b0 + 1][:], wppv[:, kc, ob * 128:(ob + 1) * 128], pTt[:, kc, :], kc == 0, kc == 1,
                       ["wpp", "pT"], [PSK(b0 + 1)])
                act(slot(s0), ps[b0][:], AF.Sigmoid, [PSK(b0)], [SK(s0)])
                tt(slot(s0), slot(s0), ps[b0 + 1][:], ALU.mult, [SK(s0), PSK(b0 + 1)], [SK(s0)])
                tt(xt[:, ob, :], xt[:, ob, :], slot(s0), ALU.add, [("x", ob), SK(s0)], [("x", ob)])

        outs = []
        outs.append(dma("sp", x_dst.rearrange("(c p) t -> p c t", p=128)[:, :, tok], xt[:], xkeys, [("xd", t)], "xout"))
        if last_layer:
            for c in range(NKC):
                s = c % 2
                act(slot(s), xt[:, c, :], AF.Square, [("x", c)], [SK(s)])
                mm(ps[7][:], ones_c, slot(s), c == 0, c == NKC - 1, [SK(s), "consts"], [PSK(7)])
            act(slot(2), ps[7][:], AF.Sqrt, [PSK(7), "epsb"], [SK(2)], bias=epsb[:, 0:1], scale=1.0 / D)
            recip(slot(3), slot(2), [SK(2)], [SK(3)])
            for c in range(NKC):
                sl = 12 + c % 4
                stt(slot(sl), xt[:, c, :], fng[:, c:c + 1], slot(3), ALU.mult, ALU.mult,
                    [("x", c), SK(3), "fng"], [SK(sl)])
                outs.append(dma("sp", yo_d[c * 128:(c + 1) * 128, tok], slot(sl), [SK(sl)], [], "yout%d" % (c % 4)))
        return outs

    finals = []
    for l in range(n_layers):
        emit_lb(l)
        last = (l == n_layers - 1)
        for t in range(n_tiles):
            src = xT_d if l == 0 else xo_d
            finals += emit_tile(l, t, src, xo_d, last and final_norm_all)
    finals.append(dma("sp", sout_d.rearrange("p (a v) -> p a v", v=128), S[:],
                      [("S", l, h) for l in range(n_layers) for h in range(NH)], [], "sout"))
    finals.append(dma("sp", hout_d.rearrange("p (a v) -> p a v", v=2), halo[:],
                      [("halo", l, c) for l in range(n_layers) for c in range(8)], [], "hout"))
    P.emit_all(final_wait_ops=finals)
    st.close()
    return nc, P


def make_common_inputs(inputs, layers):
    f32 = np.float32
    lbp = np.asarray(inputs["lb_param"], f32)
    wpk = np.stack([pack_layer(np.asarray(inputs["w_in"][l], f32), np.asarray(inputs["w_a_out"][l], f32),
                               np.asarray(inputs["w_b_out"][l], f32), np.asarray(inputs["w_o"][l], f32),
                               np.asarray(inputs["w_ple_gate"][l], f32), np.asarray(inputs["w_ple_proj"][l], f32))
                    for l in layers])
    smallp = np.stack([np.concatenate([
        fm(inputs["norm_mix_g"][l], 16), fm(inputs["ple_norm_g"][l], 16),
        np.ascontiguousarray(np.asarray(inputs["conv_w"][l], f32).reshape(3, 8, 128).transpose(2, 0, 1)).reshape(128, 24),
        fm(inputs["hg_norm_g"][l], 8)], axis=1) for l in layers]).astype(f32)
    cm = np.zeros((len(layers), 128, 4), f32)
    for i, l in enumerate(layers):
        cm[i, :, 1:l + 1] = 1.0
    return {
        "wpk": wpk, "smallp": smallp, "fng": fm(inputs["final_norm_g"], 16),
        "lbp_fm": np.ascontiguousarray(lbp.reshape(4, 8, 128).transpose(2, 0, 1)).reshape(128, 32),
        "lbp_bc": np.ascontiguousarray(np.broadcast_to(lbp.reshape(1, 4 * DR), (128, 4 * DR))),
        "cmask": cm, "consts": host_consts(),
        "s_in": np.zeros((128, len(layers) * NH * 128), f32),
        "halo_in": np.zeros((128, len(layers) * 16), f32),
    }


def kernel(**inputs):
    x = np.asarray(inputs["x"], np.float32)
    p = np.asarray(inputs["p"], np.float32)
    B, SEQ, _ = x.shape
    n_tiles = SEQ // TT
    common = make_common_inputs(inputs, list(range(DEPTH)))
    nc, _ = build_program(DEPTH, n_tiles, xs_internal=True)
    per_b = []
    for b in range(B):
        per_b.append({"xT": np.ascontiguousarray(x[b].T),
                      "pT": np.ascontiguousarray(p[:, b].transpose(0, 2, 1))})
    in_maps = []
    for c in range(8):
        m = dict(common)
        m.update(per_b[c // 4])
        in_maps.append(m)
    res = run_bass_kernel_spmd(nc, in_maps, core_ids=list(range(8)))
    out = np.stack([np.ascontiguousarray(res.results[4 * b]["yo"].T) for b in range(B)])
    return out.astype(np.float32)
```

```python
import contextlib
import numpy as np
import concourse.bass as bass
import concourse.mybir as mybir
from concourse.bass_utils import run_bass_kernel_spmd

F32 = mybir.dt.float32
BF16 = mybir.dt.bfloat16
AF = mybir.ActivationFunctionType
ALU = mybir.AluOpType

D = 2048
NKC = 16
PLE = 256
DC = 1024
DR = 1024
NH = 8
DEPTH = 4
TT = 512
EPS = 1e-6
GW = 8192
TOT = 8 * GW * 2 + 16 * 6144 + 4 * GW * 2 + 4096
EPOCH = 16000


class Prog:
    ENGINES = ("pe", "act", "dve", "pool", "sp")

    def __init__(self, nc):
        self.nc = nc
        self.ops = []
        self.last_w = {}
        self.readers = {}
        self.lane_count = {}

    def op(self, eng, emit, reads=(), writes=(), lane=None):
        idx = len(self.ops)
        is_dma = lane is not None
        deps = set()
        for k in reads:
            w = self.last_w.get(k)
            if w is not None:
                deps.add(w)
        for k in writes:
            w = self.last_w.get(k)
            if w is not None:
                deps.add(w)
            for r in self.readers.get(k, {}).values():
                deps.add(r)
        need = []
        for d in deps:
            p = self.ops[d]
            if p["dma"] or is_dma or p["eng"] != eng:
                need.append(d)
            else:
                raw = any(self.last_w.get(k) == d for k in reads)
                if raw and eng != "pe":
                    need.append(d)
        for d in need:
            self.ops[d]["signal"] = True
        rec = dict(eng=eng, emit=emit, dma=is_dma, lane=lane, need=sorted(need), signal=False)
        if is_dma:
            self.lane_count[lane] = self.lane_count.get(lane, 0) + 1
            rec["lane_idx"] = self.lane_count[lane]
        self.ops.append(rec)
        rkey = ("dma", lane) if is_dma else eng
        for k in reads:
            self.readers.setdefault(k, {})[rkey] = idx
        for k in writes:
            self.last_w[k] = idx
            self.readers[k] = {}
        return idx

    def emit_all(self, final_wait_ops=()):
        nc = self.nc
        cnt = {e: 0 for e in self.ENGINES}
        for o in self.ops:
            if o["dma"]:
                o["tok"] = (("lane", o["lane"]), 16 * o["lane_idx"])
            elif o["signal"]:
                c = cnt[o["eng"]]
                cnt[o["eng"]] = c + 1
                o["tok"] = (("eng", o["eng"], c // EPOCH), c % EPOCH + 1)
        semnames = []
        seen = set()
        for o in self.ops:
            if "tok" in o and o["tok"][0] not in seen:
                seen.add(o["tok"][0])
                semnames.append(o["tok"][0])
        sems = {}
        with contextlib.ExitStack() as st:
            for sn in semnames:
                sems[sn] = st.enter_context(nc.semaphore("s_" + "_".join(str(x) for x in sn)))
            block = st.enter_context(nc.Block())
            by_eng = {e: [o for o in self.ops if o["eng"] == e] for e in self.ENGINES}

            def run(engname, e):
                waited = {}
                maxep = {}
                for o in by_eng[engname]:
                    for d in o["need"]:
                        sn, val = self.ops[d]["tok"]
                        if waited.get(sn, 0) >= val:
                            continue
                        if sn[0] == "eng" and maxep.get(sn[1], -1) > sn[2]:
                            continue
                        e.wait_ge(sems[sn], val)
                        waited[sn] = val
                        if sn[0] == "eng":
                            maxep[sn[1]] = max(maxep.get(sn[1], -1), sn[2])
                    ins = o["emit"](e)
                    if "tok" in o:
                        sn, val = o["tok"]
                        ins.then_inc(sems[sn], 16 if o["dma"] else 1)
                if engname == "sp":
                    for d in final_wait_ops:
                        sn, val = self.ops[d]["tok"]
                        if waited.get(sn, 0) < val:
                            e.wait_ge(sems[sn], val)
                            waited[sn] = val

            @block.tensor
            def _(e):
                run("pe", e)

            @block.scalar
            def _(e):
                run("act", e)

            @block.vector
            def _(e):
                run("dve", e)

            @block.gpsimd
            def _(e):
                run("pool", e)

            @block.sync
            def _(e):
                run("sp", e)


def _kmajor(w):
    K, N = w.shape
    return np.ascontiguousarray(w.reshape(K // 128, 128, N).transpose(1, 0, 2)).reshape(128, -1)


def pack_layer(w_in, w_a, w_b, w_o, w_pg, w_pp):
    out = np.empty((128, TOT), np.float32)
    off = 0
    r = np.arange(128)
    for cb in range(8):
        cols = np.concatenate([j * 1024 + cb * 128 + r for j in range(4)])
        out[:, off:off + GW] = _kmajor(w_in[:, cols]); off += GW
    for hd in range(8):
        cols = np.concatenate([4096 + hd * 128 + r, 7168 + hd * 128 + r,
                               5120 + hd * 128 + r, 6144 + hd * 128 + r])
        out[:, off:off + GW] = _kmajor(w_in[:, cols]); off += GW
    for ob in range(16):
        cols = np.concatenate([8192 + ob * 128 + r, 10240 + ob * 128 + r])
        out[:, off:off + 4096] = _kmajor(w_in[:, cols]); off += 4096
        out[:, off:off + 1024] = _kmajor(w_a[:, ob * 128:(ob + 1) * 128]); off += 1024
        out[:, off:off + 1024] = _kmajor(w_b[:, ob * 128:(ob + 1) * 128]); off += 1024
    for g in range(4):
        out[:, off:off + GW] = _kmajor(w_o[:, g * 512:(g + 1) * 512]); off += GW
    for g in range(4):
        out[:, off:off + GW] = _kmajor(w_pg[:, g * 512:(g + 1) * 512]); off += GW
    out[:, off:off + 4096] = _kmajor(w_pp); off += 4096
    assert off == TOT
    return out


def group_table():
    g = []
    off = 0
    for cb in range(8):
        g.append(("A", cb, off, GW)); off += GW
    for hd in range(8):
        g.append(("B", hd, off, GW)); off += GW
    for ob in range(16):
        g.append(("M", ob, off, 6144)); off += 6144
    for k in range(4):
        g.append(("O", k, off, GW)); off += GW
    for k in range(4):
        g.append(("G", k, off, GW)); off += GW
    pp = ("PP", 0, off, 4096)
    return g, pp


def host_consts():
    s = np.arange(128)
    same = (s[:, None] // 64) == (s[None, :] // 64)
    md = np.where(same & (s[:, None] > s[None, :]), -1.0, 0.0).astype(np.float32)
    mask = np.where(same & (s[:, None] <= s[None, :]), 1.0, 0.0).astype(np.float32)
    csum = np.zeros((128, 2), np.float32)
    csum[:64, 0] = 1.0
    csum[64:, 1] = 1.0
    ones = np.ones((128, 128), np.float32)
    return np.concatenate([ones, md, mask, csum], axis=1)


def fm(v, n):
    return np.ascontiguousarray(np.asarray(v, np.float32).reshape(n, 128).T)


def build_program(n_layers, n_tiles, nwb=2, final_norm_all=True, xs_internal=False):
    T = n_tiles * TT
    nc = bass.Bass("TRN2", target_bir_lowering=False)
    dt_in = lambda name, shape: nc.dram_tensor(name, shape, F32, kind="ExternalInput").ap()
    dt_out = lambda name, shape: nc.dram_tensor(name, shape, F32, kind="ExternalOutput").ap()
    xT_d = dt_in("xT", [D, T])
    pT_d = dt_in("pT", [n_layers, PLE, T])
    wpk_d = dt_in("wpk", [n_layers, 128, TOT])
    smallp_d = dt_in("smallp", [n_layers, 128, 64])
    fng_d = dt_in("fng", [128, 16])
    lbfm_d = dt_in("lbp_fm", [128, 32])
    lbbc_d = dt_in("lbp_bc", [128, 4 * DR])
    cmask_d = dt_in("cmask", [n_layers, 128, 4])
    consts_d = dt_in("consts", [128, 386])
    if xs_internal:
        xo_d = nc.dram_tensor("xs", [D, T], F32, kind="Internal").ap()
    else:
        xo_d = dt_out("xo", [D, T])
    yo_d = dt_out("yo", [D, T])

    P = Prog(nc)
    st = contextlib.ExitStack()
    sb = lambda name, shape, dt=F32: st.enter_context(nc.sbuf_tensor(name, shape, dt))

    xt = sb("xt", [128, NKC, TT])
    hT = sb("hT", [128, NKC, TT], BF16)
    yaT = sb("yaT", [128, 8, TT], BF16)
    ybT = sb("ybT", [128, 8, TT], BF16)
    mT = sb("mT", [128, NKC, TT], BF16)
    pTt = sb("pTt", [128, 2, TT], BF16)
    wbufs = [sb("wb%d" % i, [128, GW], BF16) for i in range(nwb)]
    wpp = sb("wpp", [128, 4096], BF16)
    NSLOT = 18
    SLW = 520
    arena = sb("arena", [128, NSLOT, SLW])
    bfa = sb("bfa", [128, 8, TT], BF16)
    atm = sb("atm", [128, 2, 128], BF16)
    S = sb("S", [128, NH, 128])
    halo = sb("halo", [128, 8, 2])
    smallp = sb("smallp_sb", [128, n_layers, 64])
    fng = sb("fng_sb", [128, 16])
    consts = sb("consts_sb", [128, 386])
    lbfm = sb("lbfm", [128, 4, 8])
    cmask = sb("cmask_sb", [128, n_layers, 4])
    lbq = sb("lbq", [128, 6, 8])
    lbB = sb("lbB", [128, DR])
    omlB = sb("omlB", [128, DR])
    ebt = sb("ebt", [128, 2, 8])
    epsb = sb("epsb", [128, 1])
    ps = [st.enter_context(nc.psum_tensor("ps%d" % i, [128, TT], F32)) for i in range(8)]

    ones_c = consts[:, 0:128]
    md_c = consts[:, 128:256]
    mask_c = consts[:, 256:384]
    csum_c = consts[:, 384:386]

    def mm(out, lhsT, rhs, start, stop, reads, writes):
        return P.op("pe", lambda e: e.matmul(out, lhsT=lhsT, rhs=rhs, start=start, stop=stop),
                    reads, writes)

    def act(out, in_, func, reads, writes, bias=None, scale=None):
        kw = {}
        if bias is not None:
            kw["bias"] = bias
        if scale is not None:
            kw["scale"] = scale
        return P.op("act", lambda e: e.activation(out=out, in_=in_, func=func, **kw), reads, writes)

    def tt(out, in0, in1, op, reads, writes):
        return P.op("dve", lambda e: e.tensor_tensor(out=out, in0=in0, in1=in1, op=op), reads, writes)

    def ts(out, in0, s1, s2, op0, op1, reads, writes):
        if s2 is None:
            return P.op("dve", lambda e: e.tensor_scalar(out=out, in0=in0, scalar1=s1, scalar2=None, op0=op0),
                        reads, writes)
        return P.op("dve", lambda e: e.tensor_scalar(out=out, in0=in0, scalar1=s1, scalar2=s2, op0=op0, op1=op1),
                    reads, writes)

    def stt(out, in0, scalar, in1, op0, op1, reads, writes):
        return P.op("dve", lambda e: e.scalar_tensor_tensor(out=out, in0=in0, scalar=scalar, in1=in1,
                                                            op0=op0, op1=op1), reads, writes)

    def recip(out, in_, reads, writes):
        return P.op("dve", lambda e: e.reciprocal(out=out, in_=in_), reads, writes)

    def dma(q, out, in_, reads, writes, lane):
        return P.op(q, lambda e: e.dma_start(out=out, in_=in_), reads, writes, lane=lane)

    def slot(i, w=TT):
        return arena[:, i, 0:w]

    SK = lambda i: ("sl", i)
    PSK = lambda i: ("ps", i)

    dma("sp", consts[:], consts_d[:, :], [], ["consts"], "c0")
    dma("sp", smallp[:], smallp_d.rearrange("l p n -> p l n"), [], ["smallp"], "c1")
    dma("sp", fng[:], fng_d[:, :], [], ["fng"], "c2")
    dma("sp", lbfm[:], lbfm_d.rearrange("p (l h) -> p l h", l=4), [], ["lbfm"], "c3")
    dma("sp", cmask[:], cmask_d.rearrange("l p n -> p l n"), [], ["cmask"], "c4")
    P.op("dve", lambda e: e.memset(epsb[:], EPS), [], ["epsb"])

    gtab, ppg = group_table()
    stream = []
    for l in range(n_layers):
        for t in range(n_tiles):
            for (kind, idx, off, w) in gtab:
                stream.append((l, t, kind, idx, off, w))
    issued = [0]

    def issue_upto(n):
        while issued[0] < min(n, len(stream)):
            i = issued[0]
            l, t, kind, idx, off, w = stream[i]
            b = i % nwb
            dma("pool", wbufs[b][:, 0:w], wpk_d[l, :, off:off + w], [], [("wb", b)], "w%d" % b)
            issued[0] += 1

    gpos = [0]

    def next_group():
        i = gpos[0]
        issue_upto(i + nwb)
        gpos[0] += 1
        return wbufs[i % nwb], ("wb", i % nwb)

    def emit_lb(l):
        e4 = arena[:, 0, 0:32].rearrange("p (l h) -> p l h", l=4)
        act(e4, lbfm[:], AF.Exp, ["lbfm"], [SK(0)])
        ssum, num, lb_, oml_, noml_, tmp = [lbq[:, i, :] for i in range(6)]
        tt(ssum, e4[:, 0, :], e4[:, 1, :], ALU.add, [SK(0)], ["lbq0"])
        tt(ssum, ssum, e4[:, 2, :], ALU.add, [SK(0), "lbq0"], ["lbq0"])
        tt(ssum, ssum, e4[:, 3, :], ALU.add, [SK(0), "lbq0"], ["lbq0"])
        recip(ssum, ssum, ["lbq0"], ["lbq0"])
        ts(num, e4[:, 1, :], cmask[:, l, 1:2], None, ALU.mult, None, [SK(0), "cmask"], ["lbq1"])
        stt(num, e4[:, 2, :], cmask[:, l, 2:3], num, ALU.mult, ALU.add, [SK(0), "cmask", "lbq1"], ["lbq1"])
        stt(num, e4[:, 3, :], cmask[:, l, 3:4], num, ALU.mult, ALU.add, [SK(0), "cmask", "lbq1"], ["lbq1"])
        tt(lb_, num, ssum, ALU.mult, ["lbq0", "lbq1"], ["lbq2"])
        ts(lb_, lb_, 0.0, 1.0 - 1e-6, ALU.max, ALU.min, ["lbq2"], ["lbq2"])
        ts(oml_, lb_, -1.0, 1.0, ALU.mult, ALU.add, ["lbq2"], ["lbq3"])
        ts(noml_, oml_, -1.0, None, ALU.mult, None, ["lbq3"], ["lbq4"])
        big = arena[:, 2:10, :].rearrange("p a b -> p (a b)")[:, 0:4 * DR].rearrange("p (l c) -> p l c", l=4)
        bk = [SK(i) for i in range(2, 10)]
        dma("sp", big, lbbc_d.rearrange("p (l c) -> p l c", l=4), [], bk, "lbbc")
        act(big, big, AF.Exp, bk, bk)
        sB = arena[:, 10:12, :].rearrange("p a b -> p (a b)")[:, 0:DR]
        nB = arena[:, 12:14, :].rearrange("p a b -> p (a b)")[:, 0:DR]
        sk = [SK(10), SK(11)]
        nk = [SK(12), SK(13)]
        tt(sB, big[:, 0, :], big[:, 1, :], ALU.add, bk, sk)
        tt(sB, sB, big[:, 2, :], ALU.add, bk + sk, sk)
        tt(sB, sB, big[:, 3, :], ALU.add, bk + sk, sk)
        recip(sB, sB, sk, sk)
        ts(nB, big[:, 1, :], cmask[:, l, 1:2], None, ALU.mult, None, bk + ["cmask"], nk)
        stt(nB, big[:, 2, :], cmask[:, l, 2:3], nB, ALU.mult, ALU.add, bk + nk + ["cmask"], nk)
        stt(nB, big[:, 3, :], cmask[:, l, 3:4], nB, ALU.mult, ALU.add, bk + nk + ["cmask"], nk)
        tt(lbB[:], nB, sB, ALU.mult, sk + nk, ["lbB"])
        ts(lbB[:], lbB[:], 0.0, 1.0 - 1e-6, ALU.max, ALU.min, ["lbB"], ["lbB"])
        ts(omlB[:], lbB[:], -1.0, 1.0, ALU.mult, ALU.add, ["lbB"], ["omlB"])

    def emit_norm(gain_ap, out_fn, out_keys_fn):
        for c in range(NKC):
            s = c % 2
            act(slot(s), xt[:, c, :], AF.Square, [("x", c)], [SK(s)])
            mm(ps[7][:], ones_c, slot(s), c == 0, c == NKC - 1, [SK(s), "consts"], [PSK(7)])
        act(slot(2), ps[7][:], AF.Sqrt, [PSK(7), "epsb"], [SK(2)], bias=epsb[:, 0:1], scale=1.0 / D)
        recip(slot(3), slot(2), [SK(2)], [SK(3)])
        for c in range(NKC):
            stt(out_fn(c), xt[:, c, :], gain_ap[:, c:c + 1], slot(3), ALU.mult, ALU.mult,
                [("x", c), SK(3), "smallp", "fng"], out_keys_fn(c))

    def emit_tile(l, t, x_src, x_dst, last_layer):
        tok = slice(t * TT, (t + 1) * TT)
        sp_l = smallp[:, l, :]
        g_mix = sp_l[:, 0:16]
        g_ple = sp_l[:, 16:32]
        convw = sp_l[:, 32:56].rearrange("p (k c) -> p k c", k=3)
        g_hn = sp_l[:, 56:64]
        xkeys = [("x", c) for c in range(NKC)]
        hkeys = [("h", c) for c in range(NKC)]
        for c in range(NKC):
            dma("sp", xt[:, c, :], x_src[c * 128:(c + 1) * 128, tok], [("xd", t, c)], [("x", c)], "xin%d" % c)
        dma("pool", pTt[:], pT_d[l].rearrange("(c p) t -> p c t", p=128)[:, :, tok], [], ["pT"], "pin")
        emit_norm(g_mix, lambda c: hT[:, c, :], lambda c: [("h", c)])

        for cb in range(8):
            wb, wk = next_group()
            wv = wb[:, :].rearrange("p (k n) -> p k n", n=512)
            b0 = 0 if cb % 2 == 0 else 4
            so = 0 if cb % 2 == 0 else 4
            sC, sX, sT, sZ = 4 + so, 5 + so, 6 + so, 7 + so
            for j in range(4):
                for kc in range(NKC):
                    mm(ps[b0 + j][:], wv[:, kc, j * 128:(j + 1) * 128], hT[:, kc, :], kc == 0, kc == NKC - 1,
                       [wk, ("h", kc)], [PSK(b0 + j)])
            cx = arena[:, sX, 0:TT + 2]
            act(slot(sC), ps[b0 + 0][:], AF.Copy, [PSK(b0)], [SK(sC)])
            act(cx[:, 0:2], halo[:, cb, :], AF.Copy, [("halo", cb)], [SK(sX)])
            tt(cx[:, 2:TT + 2], slot(sC), ps[b0 + 2][:], ALU.mult, [SK(sC), PSK(b0 + 2)], [SK(sX)])
            act(halo[:, cb, :], cx[:, TT:TT + 2], AF.Copy, [SK(sX)], [("halo", cb)])
            ts(slot(sT), cx[:, 0:TT], convw[:, 0, cb:cb + 1], None, ALU.mult, None, [SK(sX), "smallp"], [SK(sT)])
            stt(slot(sT), cx[:, 1:TT + 1], convw[:, 1, cb:cb + 1], slot(sT), ALU.mult, ALU.add,
                [SK(sX), SK(sT), "smallp"], [SK(sT)])
            stt(slot(sT), cx[:, 2:TT + 2], convw[:, 2, cb:cb + 1], slot(sT), ALU.mult, ALU.add,
                [SK(sX), SK(sT), "smallp"], [SK(sT)])
            act(slot(sZ), ps[b0 + 3][:], AF.Silu, [PSK(b0 + 3)], [SK(sZ)])
            tt(slot(sT), slot(sT), slot(sZ), ALU.mult, [SK(sT), SK(sZ)], [SK(sT)])
            tt(yaT[:, cb, :], slot(sT), ps[b0 + 1][:], ALU.mult, [SK(sT), PSK(b0 + 1)], [("ya", cb)])

        SUB = lambda b, nm: ("ps", b, nm)
        bgroups = {}

        def grp(h):
            if h not in bgroups:
                wb_, wk_ = next_group()
                bgroups[h] = (wb_[:, :].rearrange("p (k n) -> p k n", n=512), wk_)
            return bgroups[h]

        def hv(h):
            par = h % 2
            d = dict(par=par, hs=slice(h * 128, (h + 1) * 128))
            d["bq"] = [bfa[:, par * 4 + i, :] for i in range(4)]
            d["bk"] = [("bfa", par, i) for i in range(4)]
            d["kt_bf"] = d["bq"][2].rearrange("p (a k) -> p a k", a=4)
            d["v_bf"] = d["bq"][3].rearrange("p (a k) -> p a k", a=4)
            d["sog"] = 1 if par == 0 else 13
            d["qh"] = 5 if par == 0 else 14
            d["eb"] = ebt[:, par, :]
            d["ebk"] = ("eb", par)
            return d

        def S1_T(h):
            wv, wk = grp(h)
            for tb in range(4):
                bank = 3 + tb // 2
                o_ap = ps[bank][:, (tb % 2) * 256:(tb % 2) * 256 + 256]
                for kc in range(NKC):
                    mm(o_ap, hT[:, kc, tb * 128:(tb + 1) * 128], wv[:, kc, 256:512], kc == 0, kc == NKC - 1,
                       [wk, ("h", kc)], [PSK(bank)])

        def S1_preptok(h):
            H = hv(h)
            sig_tok = slot(6).rearrange("p (a k) -> p a k", a=4)
            f_tok = slot(7).rearrange("p (a k) -> p a k", a=4)
            k_tok = slot(8).rearrange("p (a k) -> p a k", a=4)
            for half in range(2):
                pv = ps[3 + half][:, :].rearrange("p (a c) -> p a c", c=256)
                act(sig_tok[:, 2 * half:2 * half + 2, :], pv[:, :, 0:128], AF.Sigmoid, [PSK(3 + half)], [SK(6)])
                act(H["v_bf"][:, 2 * half:2 * half + 2, :], pv[:, :, 128:256], AF.Copy, [PSK(3 + half)], [H["bk"][3]])
            omlB_b = omlB[:, H["hs"]].unsqueeze(1).broadcast_to([128, 4, 128])
            lbB_b = lbB[:, H["hs"]].unsqueeze(1).broadcast_to([128, 4, 128])
            tt(sig_tok, sig_tok, omlB_b, ALU.mult, [SK(6), "omlB"], [SK(6)])
            tt(f_tok, sig_tok, lbB_b, ALU.add, [SK(6), "lbB"], [SK(7)])
            stt(k_tok, sig_tok, -1.0, omlB_b, ALU.mult, ALU.add, [SK(6), "omlB"], [SK(8)])
            act(f_tok, f_tok, AF.Ln, [SK(7)], [SK(7)])

        def S1_F(h, j):
            H = hv(h)
            wv, wk = grp(h)
            bank = 5 if j < 2 else 6
            for kc in range(NKC):
                mm(ps[bank][:], wv[:, kc, j * 128:(j + 1) * 128], hT[:, kc, :], kc == 0, kc == NKC - 1,
                   [wk, ("h", kc)], [PSK(bank)])
            if j == 0:
                act(slot(0), ps[5][:], AF.Silu, [PSK(5)], [SK(0)])
            elif j == 1:
                act(slot(H["sog"]), ps[5][:], AF.Silu, [PSK(5)], [SK(H["sog"])])
            else:
                act(slot(2), ps[6][:], AF.Sigmoid, [PSK(6)], [SK(2)])
                ts(slot(2), slot(2), lbq[:, 4, h:h + 1], lbq[:, 3, h:h + 1], ALU.mult, ALU.add,
                   [SK(2), "lbq3", "lbq4"], [SK(2)])

        def S1_d(h):
            H = hv(h)
            f_tok = slot(7).rearrange("p (a k) -> p a k", a=4)
            k_tok = slot(8).rearrange("p (a k) -> p a k", a=4)
            ek_tok = slot(9).rearrange("p (a k) -> p a k", a=4)
            for tb in range(4):
                mm(ps[3][:, tb * 128:(tb + 1) * 128], md_c, f_tok[:, tb, :], True, True, [SK(7), "consts"], [PSK(3)])
            for tb in range(4):
                mm(ps[4][:, tb * 128:(tb + 1) * 128], f_tok[:, tb, :], md_c, True, True, [SK(7), "consts"], [PSK(4)])
            for tb in range(4):
                mm(ps[5][:, tb * 2:tb * 2 + 2], f_tok[:, tb, :], csum_c, True, True,
                   [SK(7), "consts"], [PSK(5)])
            act(slot(9), ps[3][:], AF.Exp, [PSK(3)], [SK(9)], scale=-1.0)
            tt(H["kt_bf"], k_tok, ek_tok, ALU.mult, [SK(8), SK(9)], [H["bk"][2]])
            act(slot(3), ps[4][:], AF.Exp, [PSK(4)], [SK(3)])
            act(slot(4), ps[4][:], AF.Exp, [PSK(4)], [SK(4)], scale=-1.0)
            act(H["eb"], ps[5][:, 0:8], AF.Exp, [PSK(5)], [H["ebk"]])

        def S1_rest(h):
            H = hv(h)
            tt(slot(3), slot(0), slot(3), ALU.mult, [SK(0), SK(3)], [SK(3)])
            act(H["bq"][0], slot(3), AF.Copy, [SK(3)], [H["bk"][0]])
            tt(H["bq"][1], slot(2), slot(4), ALU.mult, [SK(2), SK(4)], [H["bk"][1]])
            tt(slot(H["qh"]).rearrange("p (c j) -> p c j", j=64), slot(3).rearrange("p (c j) -> p c j", j=64),
               H["eb"].unsqueeze(2).broadcast_to([128, 8, 64]), ALU.mult, [SK(3), H["ebk"]], [SK(H["qh"])])

        def S2_step(h, c):
            H = hv(h)
            tb, half = c // 2, c % 2
            tsl = slice(tb * 128, (tb + 1) * 128)
            a = tb % 2
            Sh = S[:, h, :]
            skey = ("S", h)
            if half == 0:
                mm(ps[1][:, 0:128], H["bq"][1][:, tsl], H["bq"][0][:, tsl], True, True,
                   [H["bk"][0], H["bk"][1]], [PSK(1)])
                tt(atm[:, a, :], ps[1][:, 0:128], mask_c, ALU.mult, [PSK(1), "consts"], [("atm", a)])
                mm(ps[0][:, tsl], H["v_bf"][:, tb, :], atm[:, a, :], True, False, [H["bk"][3], ("atm", a)], [PSK(0)])
            csl = slice(c * 64, (c + 1) * 64)
            prt = slice(half * 64, (half + 1) * 64)
            mm(ps[0][:, csl], Sh, slot(H["qh"])[:, csl], False, True, [skey, SK(H["qh"])], [PSK(0)])
            mm(ps[2][:, 0:128], H["kt_bf"][prt, tb, :], H["v_bf"][prt, tb, :], True, True,
               [H["bk"][2], H["bk"][3]], [PSK(2)])
            stt(Sh, Sh, H["eb"][:, c:c + 1], ps[2][:, 0:128], ALU.mult, ALU.add,
                [skey, H["ebk"], PSK(2)], [skey])

        def S2_tail(h):
            H = hv(h)
            act(slot(10), ps[0][:], AF.Square, [PSK(0)], [SK(10)])
            mm(ps[7][:], ones_c, slot(10), True, True, [SK(10), "consts"], [PSK(7)])
            act(slot(11), ps[7][:], AF.Sqrt, [PSK(7), "epsb"], [SK(11)], bias=epsb[:, 0:1], scale=1.0 / 128)
            recip(slot(11), slot(11), [SK(11)], [SK(11)])
            stt(slot(12), ps[0][:], g_hn[:, h:h + 1], slot(11), ALU.mult, ALU.mult,
                [PSK(0), SK(11), "smallp"], [SK(12)])
            tt(ybT[:, h, :], slot(12), slot(H["sog"]), ALU.mult, [SK(12), SK(H["sog"])], [("yb", h)])

        S1_T(0); S1_preptok(0); S1_F(0, 0); S1_F(0, 1); S1_F(0, 2); S1_d(0); S1_rest(0)
        for h in range(NH):
            nxt = h + 1 < NH
            if nxt:
                S1_T(h + 1)
            S2_step(h, 0); S2_step(h, 1)
            if nxt:
                S1_preptok(h + 1); S1_F(h + 1, 0)
            S2_step(h, 2); S2_step(h, 3)
            if nxt:
                S1_F(h + 1, 1)
            S2_step(h, 4); S2_step(h, 5)
            if nxt:
                S1_F(h + 1, 2); S1_d(h + 1)
            S2_step(h, 6); S2_step(h, 7)
            S2_tail(h)
            if nxt:
                S1_rest(h + 1)

        for ob in range(16):
            wb, wk = next_group()
            wg = wb[:, 0:4096].rearrange("p (k n) -> p k n", n=256)
            wa = wb[:, 4096:5120].rearrange("p (k n) -> p k n", n=128)
            wbb = wb[:, 5120:6144].rearrange("p (k n) -> p k n", n=128)
            b0 = 0 if ob % 2 == 0 else 4
            so = 0 if ob % 2 == 0 else 4
            for kc in range(NKC):
                mm(ps[b0][:], wg[:, kc, 0:128], hT[:, kc, :], kc == 0, kc == NKC - 1, [wk, ("h", kc)], [PSK(b0)])
            for kc in range(NKC):
                mm(ps[b0 + 1][:], wg[:, kc, 128:256], hT[:, kc, :], kc == 0, kc == NKC - 1,
                   [wk, ("h", kc)], [PSK(b0 + 1)])
            for kc in range(8):
                mm(ps[b0 + 2][:], wa[:, kc, :], yaT[:, kc, :], kc == 0, kc == 7, [wk, ("ya", kc)], [PSK(b0 + 2)])
            for kc in range(8):
                mm(ps[b0 + 3][:], wbb[:, kc, :], ybT[:, kc, :], kc == 0, kc == 7, [wk, ("yb", kc)], [PSK(b0 + 3)])
            s0, s1 = 4 + so, 5 + so
            act(slot(s0), ps[b0][:], AF.Sigmoid, [PSK(b0)], [SK(s0)])
            act(slot(s1), ps[b0 + 1][:], AF.Sigmoid, [PSK(b0 + 1)], [SK(s1)])
            tt(slot(s0), slot(s0), ps[b0 + 2][:], ALU.mult, [SK(s0), PSK(b0 + 2)], [SK(s0)])
            tt(slot(s1), slot(s1), ps[b0 + 3][:], ALU.mult, [SK(s1), PSK(b0 + 3)], [SK(s1)])
            tt(mT[:, ob, :], slot(s0), slot(s1), ALU.add, [SK(s0), SK(s1)], [("m", ob)])

        for g in range(4):
            wb, wk = next_group()
            wv = wb[:, :].rearrange("p (k n) -> p k n", n=512)
            for j in range(4):
                ob = g * 4 + j
                bank = ob % 8
                for kc in range(NKC):
                    mm(ps[bank][:], wv[:, kc, j * 128:(j + 1) * 128], mT[:, kc, :], kc == 0, kc == NKC - 1,
                       [wk, ("m", kc)], [PSK(bank)])
                tt(xt[:, ob, :], xt[:, ob, :], ps[bank][:], ALU.add, [("x", ob), PSK(bank)], [("x", ob)])

        outs = []
        dma("pool", wpp[:], wpk_d[l, :, ppg[2]:ppg[2] + 4096], [], ["wpp"], "wpp")
        wppv = wpp[:, :].rearrange("p (k n) -> p k n", n=2048)
        emit_norm(g_ple, lambda c: hT[:, c, :], lambda c: [("h", c)])
        for g in range(4):
            wb, wk = next_group()
            wv = wb[:, :].rearrange("p (k n) -> p k n", n=512)
            for j in range(4):
                ob = g * 4 + j
                b0 = 0 if ob % 2 == 0 else 4
                s0 = 4 if ob % 2 == 0 else 8
                for kc in range(NKC):
                    mm(ps[b0][:], wv[:, kc, j * 128:(j + 1) * 128], hT[:, kc, :], kc == 0, kc == NKC - 1,
                       [wk, ("h", kc)], [PSK(b0)])
                for kc in range(2):
                    mm(ps[b0 + 1][:], wppv[:, kc, ob * 128:(ob + 1) * 128], pTt[:, kc, :], kc == 0, kc == 1,
                       ["wpp", "pT"], [PSK(b0 + 1)])
                act(slot(s0), ps[b0][:], AF.Sigmoid, [PSK(b0)], [SK(s0)])
                tt(slot(s0), slot(s0), ps[b0 + 1][:], ALU.mult, [SK(s0), PSK(b0 + 1)], [SK(s0)])
                tt(xt[:, ob, :], xt[:, ob, :], slot(s0), ALU.add, [("x", ob), SK(s0)], [("x", ob)])
                if not last_layer:
                    outs.append(dma("sp", x_dst[ob * 128:(ob + 1) * 128, tok], xt[:, ob, :], [("x", ob)],
                                    [("xd", t, ob)], "xout%d" % ob))

        if last_layer:
            for c in range(NKC):
                s = c % 2
                act(slot(s), xt[:, c, :], AF.Square, [("x", c)], [SK(s)])
                mm(ps[7][:], ones_c, slot(s), c == 0, c == NKC - 1, [SK(s), "consts"], [PSK(7)])
            act(slot(2), ps[7][:], AF.Sqrt, [PSK(7), "epsb"], [SK(2)], bias=epsb[:, 0:1], scale=1.0 / D)
            recip(slot(3), slot(2), [SK(2)], [SK(3)])
            for c in range(NKC):
                sl = 12 + c % 4
                stt(slot(sl), xt[:, c, :], fng[:, c:c + 1], slot(3), ALU.mult, ALU.mult,
                    [("x", c), SK(3), "fng"], [SK(sl)])
                outs.append(dma("sp", yo_d[c * 128:(c + 1) * 128, tok], slot(sl), [SK(sl)], [], "yout%d" % (c % 4)))
        return outs

    finals = []
    for l in range(n_layers):
        P.op("dve", lambda e: e.memset(S[:], 0.0), [], [("S", h) for h in range(NH)])
        P.op("dve", lambda e: e.memset(halo[:], 0.0), [], [("halo", c) for c in range(8)])
        emit_lb(l)
        last = (l == n_layers - 1)
        for t in range(n_tiles):
            src = xT_d if l == 0 else xo_d
            finals += emit_tile(l, t, src, xo_d, last and final_norm_all)
    P.emit_all(final_wait_ops=finals)
    st.close()
    return nc, P


def make_common_inputs(inputs, layers):
    f32 = np.float32
    lbp = np.asarray(inputs["lb_param"], f32)
    wpk = np.stack([pack_layer(np.asarray(inputs["w_in"][l], f32), np.asarray(inputs["w_a_out"][l], f32),
                               np.asarray(inputs["w_b_out"][l], f32), np.asarray(inputs["w_o"][l], f32),
                               np.asarray(inputs["w_ple_gate"][l], f32), np.asarray(inputs["w_ple_proj"][l], f32))
                    for l in layers])
    smallp = np.stack([np.concatenate([
        fm(inputs["norm_mix_g"][l], 16), fm(inputs["ple_norm_g"][l], 16),
        np.ascontiguousarray(np.asarray(inputs["conv_w"][l], f32).reshape(3, 8, 128).transpose(2, 0, 1)).reshape(128, 24),
        fm(inputs["hg_norm_g"][l], 8)], axis=1) for l in layers]).astype(f32)
    cm = np.zeros((len(layers), 128, 4), f32)
    for i, l in enumerate(layers):
        cm[i, :, 1:l + 1] = 1.0
    return {
        "wpk": wpk, "smallp": smallp, "fng": fm(inputs["final_norm_g"], 16),
        "lbp_fm": np.ascontiguousarray(lbp.reshape(4, 8, 128).transpose(2, 0, 1)).reshape(128, 32),
        "lbp_bc": np.ascontiguousarray(np.broadcast_to(lbp.reshape(1, 4 * DR), (128, 4 * DR))),
        "cmask": cm, "consts": host_consts(),
    }


def kernel(**inputs):
    x = np.asarray(inputs["x"], np.float32)
    p = np.asarray(inputs["p"], np.float32)
    B, SEQ, _ = x.shape
    n_tiles = SEQ // TT
    common = make_common_inputs(inputs, list(range(DEPTH)))
    nc, _ = build_program(DEPTH, n_tiles, xs_internal=True)
    per_b = []
    for b in range(B):
        per_b.append({"xT": np.ascontiguousarray(x[b].T),
                      "pT": np.ascontiguousarray(p[:, b].transpose(0, 2, 1))})
    in_maps = []
    for c in range(8):
        m = dict(common)
        m.update(per_b[c // 4])
        in_maps.append(m)
    res = run_bass_kernel_spmd(nc, in_maps, core_ids=list(range(8)))
    out = np.stack([np.ascontiguousarray(res.results[4 * b]["yo"].T) for b in range(B)])
    return out.astype(np.float32)
```
